# Optimizing a Trainium2 kernel written in Bass

```python
import jax, jax.numpy as jnp
from jax import lax
import numpy as np

D_MODEL = 2048
BATCH = 4
SEQ = 4096
DEPTH = 2

HEAD_DIM = 128
MIX_WIDTH = D_MODEL
MEM_WIDTH = D_MODEL // 4
N_MEM_HEADS = MEM_WIDTH // HEAD_DIM
N_MEM = 256
LRU_WIDTH = MIX_WIDTH - MEM_WIDTH
N_LRU_BLOCKS = LRU_WIDTH // HEAD_DIM
FOX_WIDTH = MIX_WIDTH - MEM_WIDTH
N_FOX_HEADS = FOX_WIDTH // HEAD_DIM
CONV_WIDTH = 4
LRU_C = 8.0
BLOCK_Q = 128
PEER_HEADS = 8
N_KEYS = 128
N_EXPERTS = N_KEYS * N_KEYS
PEER_TOPK = 16
D_QUERY = 256
PEER_HALF = D_QUERY // 2
PEER_CHUNK = 128
N_A_LAYERS = (DEPTH + 1) // 2
N_B_LAYERS = DEPTH // 2
RMS_EPS = 1e-6

kernel_name = "yoco_rglru_fox_peer_memory"


def rmsnorm(x, g):
    xf = x.astype(jnp.float32)
    y = xf * lax.rsqrt(jnp.mean(xf * xf, axis=-1, keepdims=True) + RMS_EPS)
    return (y * g.astype(jnp.float32)).astype(x.dtype)


def causal_conv(x, w, b):
    c = x.shape[-1]
    y = lax.conv_general_dilated(x, w[:, None, :], window_strides=(1,), padding=((CONV_WIDTH - 1, 0),),
                                 dimension_numbers=('NWC', 'WIO', 'NWC'), feature_group_count=c)
    return y + b


def rg_lru(x, gate_w, gate_b, lam):
    b_, s_, c_ = x.shape
    xb = x.reshape(b_, s_, N_LRU_BLOCKS, HEAD_DIM)
    gates = (jnp.einsum('bsnc,ncg->bsng', xb, gate_w) + gate_b).astype(jnp.float32)
    r = jax.nn.sigmoid(gates[..., :HEAD_DIM]).reshape(b_, s_, c_)
    i = jax.nn.sigmoid(gates[..., HEAD_DIM:]).reshape(b_, s_, c_)
    log_a = -LRU_C * r * jax.nn.softplus(-lam.astype(jnp.float32))
    a = jnp.exp(log_a)
    u = jnp.sqrt(-jnp.expm1(2.0 * log_a)) * (i * x.astype(jnp.float32))

    def combine(left, right):
        a1, b1 = left
        a2, b2 = right
        return a1 * a2, a2 * b1 + b2

    _, h = lax.associative_scan(combine, (a, u), axis=1)
    return h.astype(x.dtype)


def mem_attention(qm, mem_n, w_kv, q_g, k_g):
    b_, s_, _ = qm.shape
    kv = mem_n @ w_kv
    k = rmsnorm(kv[..., :MEM_WIDTH].reshape(b_, N_MEM, N_MEM_HEADS, HEAD_DIM), k_g)
    v = kv[..., MEM_WIDTH:].reshape(b_, N_MEM, N_MEM_HEADS, HEAD_DIM)
    q = rmsnorm(qm.reshape(b_, s_, N_MEM_HEADS, HEAD_DIM), q_g)
    s = jnp.einsum('bshd,bmhd->bhsm', q, k).astype(jnp.float32) * (HEAD_DIM ** -0.5)
    p = jax.nn.softmax(s, axis=-1)
    o = jnp.einsum('bhsm,bmhd->bshd', p.astype(v.dtype), v)
    return o.reshape(b_, s_, MEM_WIDTH)


def shared_kv(x, norm_g, w_kvf, b_f, k_g):
    b_, s_, _ = x.shape
    z = rmsnorm(x, norm_g) @ w_kvf
    k = rmsnorm(z[..., :FOX_WIDTH].reshape(b_, s_, N_FOX_HEADS, HEAD_DIM), k_g)
    v = z[..., FOX_WIDTH:2 * FOX_WIDTH].reshape(b_, s_, N_FOX_HEADS, HEAD_DIM)
    log_f = jax.nn.log_sigmoid((z[..., 2 * FOX_WIDTH:] + b_f).astype(jnp.float32))
    c = jnp.cumsum(log_f, axis=1).transpose(0, 2, 1)
    return k, v, c


def forgetting_attention(q, k, v, c):
    b_, s_, h_, d_ = q.shape
    n_blocks = s_ // BLOCK_Q
    qb = q.reshape(b_, n_blocks, BLOCK_Q, h_, d_).transpose(1, 0, 2, 3, 4)
    cb = c.reshape(b_, h_, n_blocks, BLOCK_Q).transpose(2, 0, 1, 3)
    key_pos = jnp.arange(s_)

    def one_block(args):
        qi, ci, blk = args
        s = jnp.einsum('bqhd,bkhd->bhqk', qi, k).astype(jnp.float32) * (HEAD_DIM ** -0.5)
        s = s + ci[..., :, None] - c[:, :, None, :]
        q_pos = blk * BLOCK_Q + jnp.arange(BLOCK_Q)
        mask = key_pos[None, :] <= q_pos[:, None]
        s = jnp.where(mask, s, -jnp.inf)
        p = jax.nn.softmax(s, axis=-1)
        return jnp.einsum('bhqk,bkhd->bqhd', p.astype(v.dtype), v)

    out = lax.map(one_block, (qb, cb, jnp.arange(n_blocks)))
    return out.transpose(1, 0, 2, 3, 4).reshape(b_, s_, h_ * d_)


def peer(h, w_q, subkeys, u, v):
    b_, s_, d_ = h.shape
    t_ = b_ * s_
    hf = h.reshape(t_, d_)
    q = (hf @ w_q).reshape(t_, PEER_HEADS, 2, PEER_HALF)
    scores = jnp.einsum('thpc,hpkc->thpk', q, subkeys).astype(jnp.float32)
    top_s, top_i = lax.top_k(scores, PEER_TOPK)
    cand_s = top_s[:, :, 0, :, None] + top_s[:, :, 1, None, :]
    cand_i = top_i[:, :, 0, :, None] * N_KEYS + top_i[:, :, 1, None, :]
    best_s, best_j = lax.top_k(cand_s.reshape(t_, PEER_HEADS, PEER_TOPK * PEER_TOPK), PEER_TOPK)
    idx = jnp.take_along_axis(cand_i.reshape(t_, PEER_HEADS, PEER_TOPK * PEER_TOPK), best_j, axis=-1)
    g = jax.nn.softmax(best_s, axis=-1).astype(h.dtype)
    n_sel = PEER_HEADS * PEER_TOPK
    n_chunks = t_ // PEER_CHUNK

    def one_chunk(args):
        xc, ic, gc = args
        act = jax.nn.gelu(jnp.einsum('ced,cd->ce', u[ic], xc))
        return jnp.einsum('ce,ced->cd', gc * act, v[ic])

    out = lax.map(one_chunk, (hf.reshape(n_chunks, PEER_CHUNK, d_),
                              idx.reshape(n_chunks, PEER_CHUNK, n_sel),
                              g.reshape(n_chunks, PEER_CHUNK, n_sel)))
    return out.reshape(b_, s_, d_)


def setup_inputs(seed: int = 0) -> dict:
    key = jax.random.key(seed)
    ks = jax.random.split(key, 32)
    f32 = jnp.float32
    nrm = lambda k, shape, scale: jax.random.normal(k, shape, f32) * scale
    gain = lambda k, shape: 1.0 + 0.02 * jax.random.normal(k, shape, f32)
    a0 = jax.random.uniform(ks[8], (N_A_LAYERS, LRU_WIDTH), f32, 0.9, 0.999) ** (1.0 / LRU_C)
    return {
        "x": nrm(ks[0], (BATCH, SEQ, D_MODEL), 1.0),
        "mem": nrm(ks[1], (BATCH, N_MEM, D_MODEL), 1.0),
        "a_norm_g": gain(ks[2], (N_A_LAYERS, D_MODEL)),
        "a_w_in": nrm(ks[3], (N_A_LAYERS, D_MODEL, 2 * LRU_WIDTH + MEM_WIDTH), D_MODEL ** -0.5),
        "a_conv_w": nrm(ks[4], (N_A_LAYERS, CONV_WIDTH, LRU_WIDTH), CONV_WIDTH ** -0.5),
        "a_conv_b": nrm(ks[5], (N_A_LAYERS, LRU_WIDTH), 0.02),
        "a_gate_w": nrm(ks[6], (N_A_LAYERS, N_LRU_BLOCKS, HEAD_DIM, 2 * HEAD_DIM), HEAD_DIM ** -0.5),
        "a_gate_b": nrm(ks[7], (N_A_LAYERS, N_LRU_BLOCKS, 2 * HEAD_DIM), 0.02),
        "a_lambda": jnp.log(a0) - jnp.log1p(-a0),
        "a_w_out": nrm(ks[9], (N_A_LAYERS, MIX_WIDTH, D_MODEL), MIX_WIDTH ** -0.5),
        "s_norm_g": gain(ks[10], (D_MODEL,)),
        "s_w_kvf": nrm(ks[11], (D_MODEL, 2 * FOX_WIDTH + N_FOX_HEADS), D_MODEL ** -0.5),
        "s_b_f": 2.0 + 0.5 * jax.random.normal(ks[12], (N_FOX_HEADS,), f32),
        "s_k_norm_g": gain(ks[13], (HEAD_DIM,)),
        "b_norm_g": gain(ks[14], (N_B_LAYERS, D_MODEL)),
        "b_w_in": nrm(ks[15], (N_B_LAYERS, D_MODEL, FOX_WIDTH + MEM_WIDTH), D_MODEL ** -0.5),
        "b_q_norm_g": gain(ks[16], (N_B_LAYERS, HEAD_DIM)),
        "b_w_out": nrm(ks[17], (N_B_LAYERS, MIX_WIDTH, D_MODEL), MIX_WIDTH ** -0.5),
        "m_norm_g": gain(ks[18], (DEPTH, D_MODEL)),
        "m_w_kv": nrm(ks[19], (DEPTH, D_MODEL, 2 * MEM_WIDTH), D_MODEL ** -0.5),
        "m_q_norm_g": gain(ks[20], (DEPTH, HEAD_DIM)),
        "m_k_norm_g": gain(ks[21], (DEPTH, HEAD_DIM)),
        "p_norm_g": gain(ks[22], (DEPTH, D_MODEL)),
        "p_w_q": nrm(ks[23], (DEPTH, D_MODEL, PEER_HEADS * D_QUERY), D_MODEL ** -0.5),
        "p_subkeys": nrm(ks[24], (DEPTH, PEER_HEADS, 2, N_KEYS, PEER_HALF), PEER_HALF ** -0.5),
        "p_u": nrm(ks[25], (DEPTH, N_EXPERTS, D_MODEL), D_MODEL ** -0.5),
        "p_v": nrm(ks[26], (DEPTH, N_EXPERTS, D_MODEL), (PEER_HEADS * PEER_TOPK) ** -0.5),
    }


def reference(x, mem, a_norm_g, a_w_in, a_conv_w, a_conv_b, a_gate_w, a_gate_b, a_lambda, a_w_out,
              s_norm_g, s_w_kvf, s_b_f, s_k_norm_g, b_norm_g, b_w_in, b_q_norm_g, b_w_out,
              m_norm_g, m_w_kv, m_q_norm_g, m_k_norm_g, p_norm_g, p_w_q, p_subkeys, p_u, p_v):
    b_, s_, _ = x.shape
    k_sh = v_sh = c_sh = None
    for layer in range(DEPTH):
        mem_n = rmsnorm(mem, m_norm_g[layer])
        if layer < N_A_LAYERS:
            i = layer
            z = rmsnorm(x, a_norm_g[i]) @ a_w_in[i]
            xb = causal_conv(z[..., :LRU_WIDTH], a_conv_w[i], a_conv_b[i])
            yb = z[..., LRU_WIDTH:2 * LRU_WIDTH]
            qm = z[..., 2 * LRU_WIDTH:]
            main = rg_lru(xb, a_gate_w[i], a_gate_b[i], a_lambda[i]) * jax.nn.gelu(yb)
            w_out = a_w_out[i]
        else:
            if layer == N_A_LAYERS:
                k_sh, v_sh, c_sh = shared_kv(x, s_norm_g, s_w_kvf, s_b_f, s_k_norm_g)
            j = layer - N_A_LAYERS
            z = rmsnorm(x, b_norm_g[j]) @ b_w_in[j]
            q = rmsnorm(z[..., :FOX_WIDTH].reshape(b_, s_, N_FOX_HEADS, HEAD_DIM), b_q_norm_g[j])
            qm = z[..., FOX_WIDTH:]
            main = forgetting_attention(q, k_sh, v_sh, c_sh)
            w_out = b_w_out[j]
        mo = mem_attention(qm, mem_n, m_w_kv[layer], m_q_norm_g[layer], m_k_norm_g[layer])
        x = x + jnp.concatenate([main, mo], axis=-1) @ w_out
        x = x + peer(rmsnorm(x, p_norm_g[layer]), p_w_q[layer], p_subkeys[layer], p_u[layer], p_v[layer])
    return x
```

```python
import numpy as np
import concourse.bass as bass
import concourse.mybir as mybir
from concourse.bass_utils import run_bass_kernel_spmd

F32 = mybir.dt.float32
BF16 = mybir.dt.bfloat16
I32 = mybir.dt.int32
U32 = mybir.dt.uint32
AF = mybir.ActivationFunctionType
ALU = mybir.AluOpType
AX = mybir.AxisListType

D = 2048
NOWN = 16
EPS = 1e-6
NEG = -1e30


class Prog:
    ENGS = ["pe", "act", "dve", "pool", "sp"]

    def __init__(self, nc):
        self.nc = nc
        self.q = {e: [] for e in self.ENGS}
        self.cnt = {e: 0 for e in self.ENGS}
        self.lw = {}
        self.rs = {}
        self.seen = {e: {} for e in self.ENGS}
        self.esem = {e: nc.alloc_semaphore("sem_" + e) for e in self.ENGS if e != "sp"}
        self.dsem = {}

    def _collect(self, reads, writes):
        raw, other = [], []
        for k in reads:
            t = self.lw.get(k)
            if t is not None:
                raw.append(t)
        for k in writes:
            t = self.lw.get(k)
            if t is not None:
                other.append(t)
            other.extend(self.rs.get(k, ()))
        return raw, other

    def _waits(self, eng, raw, other):
        need = {}
        for typ, lst in (("raw", raw), ("oth", other)):
            for t in lst:
                if t[0] == "c":
                    _, e2, s = t
                    if e2 == eng and (eng == "pe" or typ != "raw"):
                        continue
                    sem, val = ("c", e2), s
                else:
                    sem, val = ("d", t[1]), self.dsem[t[1]][1] * 16
                if val > need.get(sem, 0):
                    need[sem] = val
        out = []
        for sem, val in need.items():
            if self.seen[eng].get(sem, 0) >= val:
                continue
            self.seen[eng][sem] = val
            out.append((sem, val))
        return out

    def _commit(self, ticket, reads, writes):
        for k in reads:
            self.rs.setdefault(k, []).append(ticket)
        for k in writes:
            self.lw[k] = ticket
            self.rs[k] = []

    def op(self, eng, fn, reads=(), writes=()):
        raw, other = self._collect(reads, writes)
        waits = self._waits(eng, raw, other)
        self.cnt[eng] += 1
        ticket = ("c", eng, self.cnt[eng])
        self.q[eng].append((fn, waits, ("c", eng), 1))
        self._commit(ticket, reads, writes)

    def dma(self, eng, fn, reads=(), writes=(), key=None):
        raw, other = self._collect(reads, writes)
        waits = self._waits(eng, raw, other)
        if key not in self.dsem:
            self.dsem[key] = [self.nc.alloc_semaphore("dsem_%d" % len(self.dsem)), 0]
        self.dsem[key][1] += 1
        ticket = ("d", key, self.dsem[key][1] * 16)
        self.q[eng].append((fn, waits, ("d", key), 16))
        self._commit(ticket, reads, writes)

    def barrier(self):
        allw = [(("c", e), self.cnt[e]) for e in self.ENGS if e != "sp" and self.cnt[e] > 0]
        allw += [(("d", k), v[1] * 16) for k, v in self.dsem.items()]
        for eng in self.ENGS:
            w = []
            for sem, val in allw:
                if self.seen[eng].get(sem, 0) >= val:
                    continue
                self.seen[eng][sem] = val
                w.append((sem, val))
            self.q[eng].append((None, w, None, 0))
        self.lw.clear()
        self.rs.clear()

    def _sem(self, s):
        return self.esem[s[1]] if s[0] == "c" else self.dsem[s[1]][0]

    def emit(self):
        nc = self.nc
        with nc.Block() as block:
            def run(engname):
                def body(e):
                    for fn, waits, inc, n in self.q[engname]:
                        for sem, val in waits:
                            e.wait_ge(self._sem(sem), val)
                        if fn is None:
                            continue
                        fn(e).then_inc(self._sem(inc), n)
                return body
            block.tensor(run("pe"))
            block.scalar(run("act"))
            block.vector(run("dve"))
            block.gpsimd(run("pool"))
            block.sync(run("sp"))
        return nc


def _prod(s):
    r = 1
    for v in s:
        r *= v
    return r


class Arena:
    def __init__(self, nc, nbytes):
        self.t = nc.alloc_sbuf_tensor("arena", [128, nbytes // 2], BF16)
        self.cap = nbytes // 2
        self.off = 0
        self.n = 0

    def alloc(self, shape, dtype):
        n = _prod(shape)
        ne = n if dtype == BF16 else 2 * n
        ne = (ne + 15) // 16 * 16
        assert self.off + ne <= self.cap, ("arena overflow", self.off, ne, self.cap)
        ap = self.t[:, self.off:self.off + (n if dtype == BF16 else 2 * n)]
        self.off += ne
        if dtype != BF16:
            ap = ap.bitcast(dtype)
        if len(shape) == 2:
            ap = ap.rearrange("p (a b) -> p a b", a=shape[0], b=shape[1])
        elif len(shape) == 3:
            ap = ap.rearrange("p (a b c) -> p a b c", a=shape[0], b=shape[1], c=shape[2])
        self.n += 1
        return ap


class KB:
    def __init__(self, name):
        self.nc = bass.Bass("TRN2", target_bir_lowering=False)
        self.P = Prog(self.nc)
        self.A = Arena(self.nc, 206 * 1024)
        self.PS = self.nc.alloc_psum_tensor("ps", [128, 4096], F32)
        self.din = {}
        self.dout = {}
        self.outkeys = []
        self.uid = 0

    def inp(self, name, shape, dtype=F32):
        self.din[name] = self.nc.dram_tensor(name, list(shape), dtype, kind="ExternalInput").ap()
        return self.din[name]

    def out(self, name, shape, dtype=F32):
        self.dout[name] = self.nc.dram_tensor(name, list(shape), dtype, kind="ExternalOutput").ap()
        return self.dout[name]

    def bank(self, k, n=1):
        return self.PS[:, k * 512:(k + n) * 512]

    def mm(self, out, lhsT, rhs, start, stop, r, w):
        self.P.op("pe", lambda e: e.matmul(out=out, lhsT=lhsT, rhs=rhs, start=start, stop=stop), r, w)

    def tr(self, out, in_, ident, r, w):
        self.P.op("pe", lambda e: e.transpose(out=out, in_=in_, identity=ident), r, w)

    def act(self, out, in_, func, r, w, bias=None, scale=None, accum=None):
        kw = {}
        if bias is not None:
            kw["bias"] = bias
        if scale is not None:
            kw["scale"] = scale
        if accum is not None:
            kw["accum_out"] = accum
        self.P.op("act", lambda e: e.activation(out=out, in_=in_, func=func, **kw), r, w)

    def tt(self, out, in0, in1, op, r, w, eng="dve"):
        self.P.op(eng, lambda e: e.tensor_tensor(out=out, in0=in0, in1=in1, op=op), r, w)

    def ts(self, out, in0, s1, op0, r, w, s2=None, op1=None, eng="dve", accum=None):
        kw = {}
        if op1 is not None:
            kw["op1"] = op1
        if accum is not None:
            kw["accum_out"] = accum
        self.P.op(eng, lambda e: e.tensor_scalar(out=out, in0=in0, scalar1=s1, scalar2=s2, op0=op0, **kw), r, w)

    def stt(self, out, in0, scalar, in1, op0, op1, r, w, accum=None):
        kw = {}
        if accum is not None:
            kw["accum_out"] = accum
        self.P.op("dve", lambda e: e.scalar_tensor_tensor(out=out, in0=in0, scalar=scalar, in1=in1,
                                                          op0=op0, op1=op1, **kw), r, w)

    def cp(self, out, in_, r, w, eng="dve"):
        if eng == "act":
            self.P.op("act", lambda e: e.copy(out=out, in_=in_), r, w)
        else:
            self.P.op(eng, lambda e: e.tensor_copy(out=out, in_=in_), r, w)

    def recip(self, out, in_, r, w):
        self.P.op("dve", lambda e: e.reciprocal(out=out, in_=in_), r, w)

    def dma(self, out, in_, r, w, key, eng="sp"):
        self.P.dma(eng, lambda e: e.dma_start(out=out, in_=in_), r, w, key=key)

    def memset(self, ap, val, w, eng="dve"):
        self.P.op(eng, lambda e: e.memset(ap, val), (), w)

    def consts(self):
        A = self.A
        self.iof = A.alloc([128], F32)
        self.identf = A.alloc([128], F32)
        self.ident = A.alloc([128], BF16)
        self.onesf = A.alloc([128], F32)
        self.onesb = A.alloc([128], BF16)
        self.iocol = A.alloc([128], F32)
        self.P.op("pool", lambda e: e.iota(self.iof, pattern=[[1, 128]], base=0, channel_multiplier=-1,
                                           allow_small_or_imprecise_dtypes=True), (), ["iof"])
        self.P.op("pool", lambda e: e.iota(self.iocol, pattern=[[1, 128]], base=0, channel_multiplier=0,
                                           allow_small_or_imprecise_dtypes=True), (), ["iocol"])
        self.P.op("dve", lambda e: e.tensor_single_scalar(out=self.identf, in_=self.iof, scalar=0.0,
                                                          op=ALU.is_equal), ["iof"], ["identf"])
        self.cp(self.ident, self.identf, ["identf"], ["ident"])
        self.memset(self.onesf, 1.0, ["onesf"])
        self.memset(self.onesb, 1.0, ["onesb"])

    def load_w(self, dst, src, ncols, col0, key, nchunk=16, dkey="wload"):
        for c in range(nchunk):
            for c0 in range(0, ncols, 2048):
                c1 = min(ncols, c0 + 2048)
                self.dma(dst[:, c, c0:c1], src[c * 128:(c + 1) * 128, col0 + c0:col0 + c1], (), [key],
                         key=dkey, eng="pool")

    def norm_T(self, xt, xk, gb, hnp, hnk, dstT, dstk, tslot, keep_hn=None):
        s = self.sm
        self.act(self.junk, xt, AF.Square, [xk], ["junk", "ssq"], accum=s["ssq"])
        self.act(s["rstd"], s["ssq"], AF.Sqrt, ["ssq"], ["rstd"], bias=EPS, scale=1.0 / D)
        self.recip(s["rstd"], s["rstd"], ["rstd"], ["rstd"])
        self.stt(hnp, xt, s["rstd"], gb, ALU.mult, ALU.mult, [xk, "rstd", "gb"], [hnk])
        ptv = self.bank(6, 2).bitcast(BF16).rearrange("p (a b) -> p a b", a=16, b=128)
        for half in range(2):
            pk = "pt%d" % half
            for c in range(8 * half, 8 * half + 8):
                self.tr(ptv[:, c, :], hnp[:, c * 128:(c + 1) * 128], self.ident, [hnk, "ident"], [pk])
            dst = dstT(half)
            if half == 0:
                self.cp(dst, ptv[:, 0:8, :], [pk], [dstk], eng="act")
            else:
                self.cp(dst, ptv[:, 8:16, :], [pk], [dstk], eng="dve")

    def small(self):
        A = self.A
        self.sm = {"ssq": A.alloc([1], F32), "rstd": A.alloc([1], F32)}
        self.junk = A.alloc([2048], BF16)

    def mem_kv(self, mem_d, gm_d, wkv_d, kg_d):
        A, P = self.A, self.P
        self.KnT = A.alloc([4, 256], BF16)
        self.Vm = A.alloc([2, 512], BF16)
        mark = A.off
        gb = A.alloc([2048], F32)
        xt = A.alloc([2048], F32)
        hnp = A.alloc([2048], BF16)
        memT = A.alloc([16, 256], BF16)
        W = A.alloc([16, 1024], BF16)
        kf = A.alloc([256], F32)
        sq = A.alloc([256], F32)
        rs = A.alloc([256], F32)
        kg = A.alloc([1], F32)
        self.dma(gb, gm_d.partition_broadcast(128), (), ["gb"], key="gb")
        self.dma(kg, kg_d, (), ["kg"], key="small")
        self.load_w(W, wkv_d, 1024, 0, "Wm")
        for m in range(2):
            self.dma(xt, mem_d[m * 128:(m + 1) * 128, :], (), ["mx"], key="mx")
            self.norm_T(xt, "mx", gb, hnp, "mhn",
                        lambda half, m=m: memT[:, 8 * half:8 * half + 8, m * 128:(m + 1) * 128], "memT", 0)
        for hh in range(4):
            ps = self.bank(0)[:, 0:256]
            for c in range(16):
                self.mm(ps, W[:, c, hh * 128:(hh + 1) * 128], memT[:, c, :], c == 0, c == 15, ["Wm", "memT"], ["ps0"])
            self.cp(kf, ps, ["ps0"], ["kf"], eng="act")
            self.tt(sq, kf, kf, ALU.mult, ["kf"], ["sq"])
            ps2 = self.bank(1)[:, 0:256]
            self.mm(ps2, self.onesf, sq, True, True, ["onesf", "sq"], ["ps1"])
            self.act(rs, ps2, AF.Sqrt, ["ps1"], ["rs"], bias=EPS, scale=1.0 / 128)
            self.recip(rs, rs, ["rs"], ["rs"])
            self.stt(self.KnT[:, hh, :], kf, kg, rs, ALU.mult, ALU.mult, ["kf", "kg", "rs"], ["KnT"])
        for m in range(2):
            ps = self.bank(2)
            for c in range(16):
                self.mm(ps, memT[:, c, m * 128:(m + 1) * 128], W[:, c, 512:1024], c == 0, c == 15, ["Wm", "memT"], ["ps2"])
            self.cp(self.Vm[:, m, :], ps, ["ps2"], ["Vm"], eng="act")
        P.barrier()
        A.off = mark

    def qnorm(self, qps_key, qps, qg, dst, dstk, wk):
        qf, sq, rs = wk[0], wk[1], wk[2]
        self.cp(qf, qps, [qps_key], ["qf"], eng="act")
        self.tt(sq, qf, qf, ALU.mult, ["qf"], ["sq"])
        ps2 = self.bank(4)
        self.mm(ps2, self.onesf, sq, True, True, ["onesf", "sq"], ["ps4"])
        self.act(rs, ps2, AF.Sqrt, ["ps4"], ["rs"], bias=EPS, scale=1.0 / 128)
        self.recip(rs, rs, ["rs"], ["rs"])
        self.stt(dst, qf, qg, rs, ALU.mult, ALU.mult, ["qf", "qg", "rs"], [dstk])

    def mem_attn(self, hh, qps_key, qps, qg, dst, dstk, wk):
        qf, sq, rs, qn, pT, rz = wk
        self.qnorm(qps_key, qps, qg, qn, "qn", wk)
        for mc in range(2):
            ps = self.bank(5)
            self.mm(ps, self.KnT[:, hh, mc * 128:(mc + 1) * 128], qn, True, True, ["KnT", "qn"], ["ps5"])
            self.act(pT[:, mc, :], ps, AF.Exp, ["ps5"], ["pT"])
        po = self.bank(4)
        for mc in range(2):
            self.mm(po, self.Vm[:, mc, hh * 128:(hh + 1) * 128], pT[:, mc, :], mc == 0, mc == 1, ["Vm", "pT"], ["ps4"])
        pz = self.bank(5)
        for mc in range(2):
            self.mm(pz, self.onesb, pT[:, mc, :], mc == 0, mc == 1, ["onesb", "pT"], ["ps5"])
        self.recip(rz, pz, ["ps5"], ["rz"])
        self.tt(dst, po, rz, ALU.mult, ["ps4", "rz"], [dstk])

    def gelu(self, dst, src, srck, dstk, wk):
        xf, t1 = wk
        self.cp(xf, src, [srck], ["gx"], eng="act")
        self.tt(t1, xf, xf, ALU.mult, ["gx"], ["gt"])
        self.ts(t1, t1, 0.0713548162726, ALU.mult, ["gt"], ["gt"], s2=1.5957691216, op1=ALU.add)
        self.tt(t1, t1, xf, ALU.mult, ["gt", "gx"], ["gt"])
        self.act(t1, t1, AF.Sigmoid, ["gt"], ["gt"])
        self.tt(dst, t1, xf, ALU.mult, ["gt", "gx"], [dstk])

    def out_proj(self, AB, wout_d, xres_d, xout_d, xres_key):
        A = self.A
        mark = A.off
        W = A.alloc([16, 2048], BF16)
        xt = A.alloc([2, 2048], F32)
        xo = A.alloc([2, 2048], F32)
        self.load_w(W, wout_d, 2048, 0, "Wo")
        for i in range(NOWN):
            s = i % 2
            self.dma(xt[:, s, :], xres_d[i * 128:(i + 1) * 128, :], [xres_key], ["xr%d" % s], key="xr%d" % s)
            for dc in range(4):
                ps = self.bank(dc)
                for k in range(16):
                    self.mm(ps, AB[:, k, i * 128:(i + 1) * 128], W[:, k, dc * 512:(dc + 1) * 512], k == 0, k == 15,
                            ["AB", "Wo"], ["ps%d" % dc])
                self.tt(xo[:, s, dc * 512:(dc + 1) * 512], ps, xt[:, s, dc * 512:(dc + 1) * 512], ALU.add,
                        ["ps%d" % dc, "xr%d" % s], ["xo%d" % s])
            self.dma(xout_d[i * 128:(i + 1) * 128, :], xo[:, s, :], ["xo%d" % s], ["x1d"], key="xo%d" % s)
        self.P.barrier()
        A.off = mark

    def peer(self, base, xin_d, xin_key, xout_d, gp_d, wq_d, skT_d, u_d, v_d, dbg=None):
        A, P = self.A, self.P
        A.off = base
        mark = A.off
        self.uid += 1
        hn_d = self.nc.dram_tensor("hn_scr%d" % self.uid, [2048, D], BF16).ap()
        idxT = A.alloc([2048], I32)
        gT = A.alloc([2048], F32)
        gb = A.alloc([2048], F32)
        self.dma(gb, gp_d.partition_broadcast(128), (), ["gb"], key="gb")
        mark2 = A.off
        W = A.alloc([16, 2048], BF16)
        skT = A.alloc([2048], BF16)
        xt = A.alloc([2, 2048], F32)
        hnp = A.alloc([2, 2048], BF16)
        hnT = A.alloc([16, 512], BF16)
        qT = A.alloc([16, 512], BF16)
        sc = A.alloc([16, 128], F32)
        sc2 = A.alloc([256], F32)
        ts1 = A.alloc([16, 16], F32)
        ti1 = A.alloc([16, 16], U32)
        ti1f = A.alloc([16, 16], F32)
        cand = A.alloc([8, 256], F32)
        bs = A.alloc([8, 16], F32)
        bj = A.alloc([8, 16], U32)
        aj = A.alloc([8, 16], U32)
        af = A.alloc([2, 8, 16], F32)
        io16 = A.alloc([8, 16, 16], F32)
        eq = A.alloc([8, 16, 16], F32)
        sel = A.alloc([2, 8, 16], F32)
        ef = A.alloc([128], F32)
        gf = A.alloc([8, 16], F32)
        rsum = A.alloc([8], F32)
        self.load_w(W, wq_d, 2048, 0, "Wq")
        self.dma(skT, skT_d, (), ["skT"], key="small", eng="pool")
        P.op("pool", lambda e: e.iota(io16, pattern=[[0, 8], [0, 16], [1, 16]], base=0, channel_multiplier=0,
                                      allow_small_or_imprecise_dtypes=True), (), ["io16"])
        for q in range(4):
            for k in range(4):
                i = 4 * q + k
                s = i % 2
                self.dma(xt[:, s, :], xin_d[i * 128:(i + 1) * 128, :], [xin_key], ["px%d" % s], key="px%d" % s)
                self.norm_T(xt[:, s, :], "px%d" % s, gb, hnp[:, s, :], "hnp%d" % s,
                            lambda half, k=k: hnT[:, 8 * half:8 * half + 8, k * 128:(k + 1) * 128], "hnT", 0)
                self.dma(hn_d[i * 128:(i + 1) * 128, :], hnp[:, s, :], ["hnp%d" % s], ["hnd"], key="hnd")
            for hp in range(16):
                ps = self.bank(hp % 4)
                for c in range(16):
                    self.mm(ps, W[:, c, hp * 128:(hp + 1) * 128], hnT[:, c, :], c == 0, c == 15, ["Wq", "hnT"],
                            ["ps%d" % (hp % 4)])
                self.cp(qT[:, hp, :], ps, ["ps%d" % (hp % 4)], ["qT"], eng="act" if hp % 2 else "dve")
            for k in range(4):
                i = 4 * q + k
                tsl = slice(i * 128, (i + 1) * 128)
                for hp in range(16):
                    bk = hp // 4
                    self.mm(self.bank(bk)[:, (hp % 4) * 128:(hp % 4 + 1) * 128], qT[:, hp, k * 128:(k + 1) * 128],
                            skT[:, hp * 128:(hp + 1) * 128], True, True, ["qT", "skT"], ["ps%d" % bk])
                for bk in range(4):
                    self.cp(sc[:, 4 * bk:4 * bk + 4, :], self.bank(bk).rearrange("p (a b) -> p a b", a=4, b=128),
                            ["ps%d" % bk], ["sc"], eng="act")
                for hp in range(16):
                    P.op("dve", lambda e, hp=hp: e.max(out=ts1[:, hp, 0:8], in_=sc[:, hp, :]), ["sc"], ["ts1"])
                    P.op("dve", lambda e, hp=hp: e.match_replace(out=sc2[:, 0:128], in_to_replace=ts1[:, hp, 0:8],
                                                                 in_values=sc[:, hp, :], imm_value=NEG),
                         ["sc", "ts1"], ["sc2"])
                    P.op("dve", lambda e, hp=hp: e.max(out=ts1[:, hp, 8:16], in_=sc2[:, 0:128]), ["sc2"], ["ts1"])
                    P.op("dve", lambda e, hp=hp: e.max_index(out=ti1[:, hp, 0:8], in_max=ts1[:, hp, 0:8],
                                                             in_values=sc[:, hp, :]), ["sc", "ts1"], ["ti1"])
                    P.op("dve", lambda e, hp=hp: e.max_index(out=ti1[:, hp, 8:16], in_max=ts1[:, hp, 8:16],
                                                             in_values=sc[:, hp, :]), ["sc", "ts1"], ["ti1"])
                self.cp(ti1f, ti1, ["ti1"], ["ti1f"])
                for h in range(8):
                    self.tt(cand[:, h, :].rearrange("p (a b) -> p a b", a=16, b=16),
                            ts1[:, 2 * h, :].unsqueeze(2).to_broadcast([128, 16, 16]),
                            ts1[:, 2 * h + 1, :].unsqueeze(1).to_broadcast([128, 16, 16]), ALU.add,
                            ["ts1"], ["cand"])
                for h in range(8):
                    P.op("dve", lambda e, h=h: e.max(out=bs[:, h, 0:8], in_=cand[:, h, :]), ["cand"], ["bs"])
                    P.op("dve", lambda e, h=h: e.match_replace(out=sc2, in_to_replace=bs[:, h, 0:8],
                                                               in_values=cand[:, h, :], imm_value=NEG),
                         ["cand", "bs"], ["sc2"])
                    P.op("dve", lambda e, h=h: e.max(out=bs[:, h, 8:16], in_=sc2), ["sc2"], ["bs"])
                    P.op("dve", lambda e, h=h: e.max_index(out=bj[:, h, 0:8], in_max=bs[:, h, 0:8],
                                                           in_values=cand[:, h, :]), ["cand", "bs"], ["bj"])
                    P.op("dve", lambda e, h=h: e.max_index(out=bj[:, h, 8:16], in_max=bs[:, h, 8:16],
                                                           in_values=cand[:, h, :]), ["cand", "bs"], ["bj"])
                self.ts(aj, bj, 4, ALU.logical_shift_right, ["bj"], ["aj"])
                self.cp(af[:, 0], aj, ["aj"], ["af"])
                self.ts(aj, bj, 15, ALU.bitwise_and, ["bj", "af"], ["aj"])
                self.cp(af[:, 1], aj, ["aj"], ["af"])
                ti1v = ti1f.rearrange("p (h two) k -> p h two k", two=2)
                for p2 in range(2):
                    self.tt(eq, io16, af[:, p2].unsqueeze(3).to_broadcast([128, 8, 16, 16]), ALU.is_equal,
                            ["io16", "af"], ["eq"])
                    self.tt(eq, eq, ti1v[:, :, p2, :].unsqueeze(2).to_broadcast([128, 8, 16, 16]), ALU.mult,
                            ["eq", "ti1f"], ["eq"])
                    P.op("dve", lambda e, p2=p2: e.tensor_reduce(out=sel[:, p2], in_=eq, axis=AX.X, op=ALU.add),
                         ["eq"], ["sel"])
                self.stt(ef.rearrange("p (h k) -> p h k", h=8), sel[:, 0], 128.0, sel[:, 1], ALU.mult, ALU.add,
                         ["sel"], ["ef"])
                self.tt(gf, bs, bs[:, :, 0:1].to_broadcast([128, 8, 16]), ALU.subtract, ["bs"], ["gf"])
                self.act(gf, gf, AF.Exp, ["gf"], ["gf"])
                P.op("dve", lambda e: e.tensor_reduce(out=rsum, in_=gf, axis=AX.X, op=ALU.add), ["gf"], ["rsum"])
                self.recip(rsum, rsum, ["rsum"], ["rsum"])
                self.tt(gf, gf, rsum.unsqueeze(2).to_broadcast([128, 8, 16]), ALU.mult, ["gf", "rsum"], ["gf"])
                pe_ = self.bank(4)[:, 0:128]
                pg_ = self.bank(5)[:, 0:128]
                self.tr(pe_, ef, self.identf, ["ef", "identf"], ["ps4"])
                self.tr(pg_, gf.rearrange("p h k -> p (h k)"), self.identf, ["gf", "identf"], ["ps5"])
                self.cp(idxT[:, tsl], pe_, ["ps4"], ["idxT"], eng="act")
                self.cp(gT[:, tsl], pg_, ["ps5"], ["gT"], eng="act")
        if dbg is not None:
            self.dma(dbg["idxT"], idxT, ["idxT"], [], key="dbg")
            self.dma(dbg["gT"], gT, ["gT"], [], key="dbg")
        P.barrier()
        A.off = mark2
        NB = 3
        Ug = A.alloc([NB, 2048], F32)
        Vg = A.alloc([NB, 2048], F32)
        actT = A.alloc([128], F32)
        wT = A.alloc([128], F32)
        Mt = A.alloc([2, 128], F32)
        xt = A.alloc([2048], F32)
        xo = A.alloc([2048], F32)
        g2 = (A.alloc([128], F32), A.alloc([128], F32))
        jf = A.alloc([2048], F32)
        hnt = A.alloc([2, 2048], BF16)
        pb = self.bank(4, 4)
        pv = self.bank(0, 4)
        nu = 0
        nv = 0
        for i in range(NOWN):
            self.dma(xt, xin_d[i * 128:(i + 1) * 128, :], [xin_key], ["pxr"], key="pxr")
            hs = i % 2
            self.dma(hnt[:, hs, :], hn_d[i * 128:(i + 1) * 128, :], ["hnd"], ["hnt%d" % hs], key="hnt%d" % hs)
            for t in range(128):
                tok = i * 128 + t
                s = nu % NB
                nu += 1
                P.dma("pool", lambda e, s=s, tok=tok: e.indirect_dma_start(
                    out=Ug[:, s, :], out_offset=None, in_=u_d,
                    in_offset=bass.IndirectOffsetOnAxis(ap=idxT[:, tok:tok + 1], axis=0)),
                    ["idxT"], ["Ug%d" % s], key="Ug%d" % s)
                for dk in range(4):
                    self.mm(pb[:, dk * 512:(dk + 1) * 512], self.ident[:, t:t + 1].to_broadcast([128, 128]),
                            hnt[:, hs, dk * 512:(dk + 1) * 512], True, True, ["ident", "hnt%d" % hs], ["pb"])
                self.stt(jf, Ug[:, s, :], 1.0, pb, ALU.mult, ALU.mult, ["Ug%d" % s, "pb"], ["jf", "actT"],
                         accum=actT[:, t:t + 1])
            self.gelu(wT, actT, "actT", "wT", g2)
            self.tt(wT, wT, gT[:, i * 128:(i + 1) * 128], ALU.mult, ["wT", "gT"], ["wT"])
            for t in range(128):
                tok = i * 128 + t
                s = nv % NB
                nv += 1
                P.dma("pool", lambda e, s=s, tok=tok: e.indirect_dma_start(
                    out=Vg[:, s, :], out_offset=None, in_=v_d,
                    in_offset=bass.IndirectOffsetOnAxis(ap=idxT[:, tok:tok + 1], axis=0)),
                    ["idxT"], ["Vg%d" % s], key="Vg%d" % s)
                ms = t % 2
                self.ts(Mt[:, ms, :], self.iocol, float(t), ALU.is_equal, ["iocol", "wT"], ["Mt%d" % ms],
                        s2=wT[:, t:t + 1], op1=ALU.mult)
                for dk in range(4):
                    self.mm(pv[:, dk * 512:(dk + 1) * 512], Mt[:, ms, :], Vg[:, s, dk * 512:(dk + 1) * 512],
                            t == 0, t == 127, ["Mt%d" % ms, "Vg%d" % s], ["pv"])
            self.tt(xo, pv, xt, ALU.add, ["pv", "pxr"], ["pxo"])
            self.dma(xout_d[i * 128:(i + 1) * 128, :], xo, ["pxo"], ["x2d"], key="pxo")
        P.barrier()
        A.off = mark

    def finish(self, keys):
        self.P.barrier()
        self.P.emit()
        return self.nc


def build_l1(debug=False):
    K = KB("l1")
    nc, P, A = K.nc, K.P, K.A
    xfull = K.inp("xfull", [4096, D])
    xown = K.inp("xown", [2048, D])
    mem = K.inp("mem", [256, D])
    hsel_d = K.inp("hsel", [128, 1])
    ga_d = K.inp("a_norm_g", [D])
    w_in = K.inp("a_w_in", [D, 3584])
    convw_d = K.inp("conv_w", [128, 12 * 4])
    convb_d = K.inp("conv_b", [128, 12])
    gw_d = K.inp("gate_w", [128, 12 * 256])
    gbias_d = K.inp("gate_b", [128, 24])
    lam_d = K.inp("lam", [128, 12])
    w_out = K.inp("a_w_out", [D, D])
    gm_d = K.inp("m_norm_g", [D])
    wkv_d = K.inp("m_w_kv", [D, 1024])
    mqg_d = K.inp("m_q_g", [128, 1])
    mkg_d = K.inp("m_k_g", [128, 1])
    gp_d = K.inp("p_norm_g", [D])
    wq_d = K.inp("p_w_q", [D, D])
    skT_d = K.inp("p_skT", [128, 2048])
    u_d = K.inp("p_u", [16384, D])
    v_d = K.inp("p_v", [16384, D])
    gs_d = K.inp("s_norm_g", [D])
    wkvf_d = K.inp("s_w_kvf", [D, 3084])
    bf_d = K.inp("s_b_f", [12])
    skg_d = K.inp("s_k_g", [128, 1])
    x1_d = K.out("x1", [2048, D])
    x2_d = K.out("x2", [2048, D])
    kT_d = K.out("kT", [12, 128, 2048], BF16)
    v_o = K.out("v", [2048, 1536], BF16)
    lf_o = K.out("lf", [2048, 12])
    dbg = None
    if debug:
        dbg = {"idxT": K.out("d_idxT", [128, 2048], I32), "gT": K.out("d_gT", [128, 2048]),
               "main": K.out("d_main", [128, 16 * 2048], BF16)}

    K.consts()
    K.small()
    hsel = A.alloc([1], F32)
    mqg = A.alloc([1], F32)
    K.dma(hsel, hsel_d, (), ["hsel"], key="small")
    K.dma(mqg, mqg_d, (), ["mqg"], key="small")
    K.ts(mqg, mqg, 128.0 ** -0.5, ALU.mult, ["mqg"], ["qg"])
    K.mem_kv(mem, gm_d, wkv_d, mkg_d)
    ab_base = A.off
    AB = A.alloc([16, 2048], BF16)
    base = A.off

    W1 = A.alloc([16, 2048], BF16)
    gb = A.alloc([2048], F32)
    xt = A.alloc([2, 2048], F32)
    hnp = A.alloc([2, 2048], BF16)
    hnT = A.alloc([16, 512], BF16)
    wk = (A.alloc([512], F32), A.alloc([512], F32), A.alloc([512], F32), A.alloc([512], BF16),
          A.alloc([2, 512], BF16), A.alloc([512], F32))
    g2 = (A.alloc([512], F32), A.alloc([512], F32))
    K.dma(gb, ga_d.partition_broadcast(128), (), ["gb"], key="gb")
    K.load_w(W1, w_in, 2048, 1536, "W1")
    for q in range(4):
        for k in range(4):
            i = 4 * q + k
            s = i % 2
            K.dma(xt[:, s, :], xown[i * 128:(i + 1) * 128, :], (), ["x%d" % s], key="x%d" % s)
            K.norm_T(xt[:, s, :], "x%d" % s, gb, hnp[:, s, :], "hnp%d" % s,
                     lambda half, k=k: hnT[:, 8 * half:8 * half + 8, k * 128:(k + 1) * 128], "hnT", 0)
        for n in range(16):
            bk = n % 4
            ps = K.bank(bk)
            for c in range(16):
                K.mm(ps, W1[:, c, n * 128:(n + 1) * 128], hnT[:, c, :], c == 0, c == 15, ["W1", "hnT"], ["ps%d" % bk])
            dst = AB[:, n, q * 512:(q + 1) * 512]
            if n < 12:
                K.gelu(dst, ps, "ps%d" % bk, "AB", g2)
            else:
                K.mem_attn(n - 12, "ps%d" % bk, ps, mqg, dst, "AB", wk)
    P.barrier()
    A.off = base

    W2 = A.alloc([16, 1536], BF16)
    gb = A.alloc([2048], F32)
    xt = A.alloc([2, 2048], F32)
    hnp = A.alloc([2, 2048], BF16)
    hnT = A.alloc([16, 512], BF16)
    GW = A.alloc([12, 256], BF16)
    cw = A.alloc([12, 4], F32)
    cb = A.alloc([12], F32)
    gbias = A.alloc([24], F32)
    coef = A.alloc([12], F32)
    lamt = A.alloc([12], F32)
    yv = A.alloc([12], F32)
    sp1 = A.alloc([12], F32)
    sp2 = A.alloc([12], F32)
    msk = A.alloc([12], U32)
    halo = A.alloc([12, 3], F32)
    hst = A.alloc([12], F32)
    zx = A.alloc([516], F32)
    xb = A.alloc([512], F32)
    xbb = A.alloc([512], BF16)
    rr = A.alloc([512], F32)
    ii = A.alloc([512], F32)
    aa = A.alloc([512], F32)
    a2 = A.alloc([512], F32)
    uu = A.alloc([512], F32)
    hh_ = A.alloc([512], F32)
    dd = A.alloc([256], F32)
    K.dma(gb, ga_d.partition_broadcast(128), (), ["gb"], key="gb")
    K.load_w(W2, w_in, 1536, 0, "W2")
    K.dma(GW, gw_d.rearrange("p (n g) -> p n g", n=12), (), ["GW"], key="small", eng="pool")
    K.dma(cw, convw_d.rearrange("p (n k) -> p n k", n=12), (), ["cw"], key="small")
    K.dma(cb, convb_d, (), ["cb"], key="small")
    K.dma(gbias, gbias_d, (), ["gbias"], key="small")
    K.dma(lamt, lam_d, (), ["lamt"], key="small")
    K.memset(halo, 0.0, ["halo"])
    K.memset(hst, 0.0, ["hst"])
    K.act(yv, lamt, AF.Exp, ["lamt"], ["yv"], scale=-1.0)
    K.ts(sp1, yv, -1.0 / 6, ALU.mult, ["yv"], ["sp1"], s2=1.0 / 5, op1=ALU.add)
    for cst in (-1.0 / 4, 1.0 / 3, -1.0 / 2, 1.0):
        K.tt(sp1, sp1, yv, ALU.mult, ["sp1", "yv"], ["sp1"])
        K.ts(sp1, sp1, cst, ALU.add, ["sp1"], ["sp1"])
    K.tt(sp1, sp1, yv, ALU.mult, ["sp1", "yv"], ["sp1"])
    K.act(sp2, yv, AF.Ln, ["yv"], ["sp2"], bias=1.0, scale=1.0)
    K.ts(msk, yv, 0.1, ALU.is_lt, ["yv"], ["msk"])
    P.op("dve", lambda e: e.copy_predicated(out=sp2, mask=msk, data=sp1), ["msk", "sp1", "sp2"], ["sp2"])
    K.ts(coef, sp2, -8.0, ALU.mult, ["sp2"], ["coef"])
    for g in range(8):
        for k in range(4):
            j = 4 * g + k
            s = j % 2
            K.dma(xt[:, s, :], xfull[j * 128:(j + 1) * 128, :], (), ["x%d" % s], key="x%d" % s)
            K.norm_T(xt[:, s, :], "x%d" % s, gb, hnp[:, s, :], "hnp%d" % s,
                     lambda half, k=k: hnT[:, 8 * half:8 * half + 8, k * 128:(k + 1) * 128], "hnT", 0)
        for n in range(12):
            bk = n % 2
            ps = K.bank(bk)
            for c in range(16):
                K.mm(ps, W2[:, c, n * 128:(n + 1) * 128], hnT[:, c, :], c == 0, c == 15, ["W2", "hnT"], ["ps%d" % bk])
            K.cp(zx[:, 0:3], halo[:, n, :], ["halo"], ["zx"], eng="pool")
            K.cp(zx[:, 3:515], ps, ["ps%d" % bk], ["zx"], eng="act")
            K.cp(halo[:, n, :], zx[:, 512:515], ["zx"], ["halo"], eng="pool")
            K.ts(xb, zx[:, 3:515], cw[:, n, 3:4], ALU.mult, ["zx", "cw", "cb"], ["xb"], s2=cb[:, n:n + 1], op1=ALU.add)
            for kk in (2, 1, 0):
                K.stt(xb, zx[:, kk:kk + 512], cw[:, n, kk:kk + 1], xb, ALU.mult, ALU.add, ["zx", "cw", "xb"], ["xb"])
            K.cp(xbb, xb, ["xb"], ["xbb"], eng="pool")
            pr = K.bank(2)
            pi = K.bank(3)
            K.mm(pr, GW[:, n, 0:128], xbb, True, True, ["GW", "xbb"], ["ps2"])
            K.mm(pi, GW[:, n, 128:256], xbb, True, True, ["GW", "xbb"], ["ps3"])
            K.act(rr, pr, AF.Sigmoid, ["ps2", "gbias"], ["rr"], bias=gbias[:, 2 * n:2 * n + 1])
            K.act(ii, pi, AF.Sigmoid, ["ps3", "gbias"], ["ii"], bias=gbias[:, 2 * n + 1:2 * n + 2])
            K.act(aa, rr, AF.Exp, ["rr", "coef"], ["aa"], scale=coef[:, n:n + 1])
            K.tt(a2, aa, aa, ALU.mult, ["aa"], ["a2"])
            K.ts(a2, a2, 0.99999994, ALU.min, ["a2"], ["a2"], s2=-1.0, op1=ALU.mult)
            K.act(a2, a2, AF.Sqrt, ["a2"], ["a2"], bias=1.0, scale=1.0)
            K.tt(uu, ii, xb, ALU.mult, ["ii", "xb"], ["uu"])
            K.tt(uu, uu, a2, ALU.mult, ["uu", "a2"], ["uu"])
            P.op("dve", lambda e, n=n: e.tensor_tensor_scan(out=hh_, data0=aa, data1=uu, initial=hst[:, n:n + 1],
                                                           op0=ALU.mult, op1=ALU.add),
                 ["aa", "uu", "hst"], ["hh"])
            K.cp(hst[:, n:n + 1], hh_[:, 511:512], ["hh"], ["hst"], eng="pool")
            hv = hh_.rearrange("p (a two t) -> p a two t", a=2, two=2)
            ddv = dd.rearrange("p (a t) -> p a t", a=2)
            K.tt(ddv, hv[:, :, 1, :], hv[:, :, 0, :], ALU.subtract, ["hh"], ["dd"])
            K.stt(ddv, ddv, hsel, hv[:, :, 0, :], ALU.mult, ALU.add, ["dd", "hsel", "hh"], ["dd"])
            dst = AB[:, n, g * 256:(g + 1) * 256]
            K.tt(dst, dd, dst, ALU.mult, ["dd", "AB"], ["AB"])
    if debug:
        K.dma(dbg["main"], AB.rearrange("p a b -> p (a b)"), ["AB"], [], key="dbg")
    P.barrier()
    A.off = base

    K.out_proj(AB, w_out, xown, x1_d, "none")
    K.peer(ab_base, x1_d, "x1d", x2_d, gp_d, wq_d, skT_d, u_d, v_d, dbg)
    A.off = base
    gb = A.alloc([2048], F32)
    xt = A.alloc([2, 2048], F32)
    hnp = A.alloc([2, 2048], BF16)
    hnT = A.alloc([16, 512], BF16)
    W = A.alloc([16, 1548], BF16)
    kf = A.alloc([512], F32)
    sq = A.alloc([512], F32)
    rs = A.alloc([512], F32)
    ko = A.alloc([2, 512], BF16)
    vo = A.alloc([2, 1536], BF16)
    lfo = A.alloc([2, 12], F32)
    lft = A.alloc([12], F32)
    bfb = A.alloc([12], F32)
    skg = A.alloc([1], F32)
    K.dma(gb, gs_d.partition_broadcast(128), (), ["gb"], key="gb")
    K.dma(bfb, bf_d.partition_broadcast(128), (), ["bfb"], key="small")
    K.dma(skg, skg_d, (), ["skg"], key="small")
    for part in range(2):
        if part == 0:
            K.load_w(W, wkvf_d, 1536, 0, "Wk")
        else:
            K.load_w(W, wkvf_d, 1548, 1536, "Wk")
        for q in range(4):
            for k in range(4):
                i = 4 * q + k
                s = i % 2
                K.dma(xt[:, s, :], x2_d[i * 128:(i + 1) * 128, :], ["x2d"], ["x%d" % s], key="x%d" % s)
                K.norm_T(xt[:, s, :], "x%d" % s, gb, hnp[:, s, :], "hnp%d" % s,
                         lambda half, k=k: hnT[:, 8 * half:8 * half + 8, k * 128:(k + 1) * 128], "hnT", 0)
            if part == 0:
                for hh in range(12):
                    bk = hh % 2
                    ps = K.bank(bk)
                    for c in range(16):
                        K.mm(ps, W[:, c, hh * 128:(hh + 1) * 128], hnT[:, c, :], c == 0, c == 15, ["Wk", "hnT"],
                             ["ps%d" % bk])
                    K.cp(kf, ps, ["ps%d" % bk], ["kf"], eng="act")
                    K.tt(sq, kf, kf, ALU.mult, ["kf"], ["sq"])
                    ps2 = K.bank(2)
                    K.mm(ps2, K.onesf, sq, True, True, ["onesf", "sq"], ["ps2"])
                    K.act(rs, ps2, AF.Sqrt, ["ps2"], ["rs"], bias=EPS, scale=1.0 / 128)
                    K.recip(rs, rs, ["rs"], ["rs"])
                    s2 = hh % 2
                    K.stt(ko[:, s2, :], kf, skg, rs, ALU.mult, ALU.mult, ["kf", "skg", "rs"], ["ko%d" % s2])
                    K.dma(kT_d[hh, :, q * 512:(q + 1) * 512], ko[:, s2, :], ["ko%d" % s2], [], key="ko%d" % s2)
            else:
                for k in range(4):
                    i = 4 * q + k
                    s2 = i % 2
                    for vc in range(3):
                        ps = K.bank(vc)
                        for c in range(16):
                            K.mm(ps, hnT[:, c, k * 128:(k + 1) * 128], W[:, c, vc * 512:(vc + 1) * 512], c == 0, c == 15,
                                 ["Wk", "hnT"], ["ps%d" % vc])
                        K.cp(vo[:, s2, vc * 512:(vc + 1) * 512], ps, ["ps%d" % vc], ["vo%d" % s2],
                             eng="act" if vc % 2 else "dve")
                    K.dma(v_o[i * 128:(i + 1) * 128, :], vo[:, s2, :], ["vo%d" % s2], [], key="vo%d" % s2)
                    ps = K.bank(3)[:, 0:12]
                    for c in range(16):
                        K.mm(ps, hnT[:, c, k * 128:(k + 1) * 128], W[:, c, 1536:1548], c == 0, c == 15, ["Wk", "hnT"], ["ps3"])
                    K.tt(lft, ps, bfb, ALU.add, ["ps3", "bfb"], ["lft"])
                    K.act(lft, lft, AF.Exp, ["lft"], ["lft"], scale=-1.0)
                    K.act(lft, lft, AF.Ln, ["lft"], ["lft"], bias=1.0, scale=1.0)
                    K.ts(lfo[:, s2, :], lft, -1.0, ALU.mult, ["lft"], ["lfo%d" % s2])
                    K.dma(lf_o[i * 128:(i + 1) * 128, :], lfo[:, s2, :], ["lfo%d" % s2], [], key="lfo%d" % s2)
        P.barrier()
    return K.finish([])


def l1_inputs(inp, c):
    b, h = c // 2, c % 2
    x = inp["x"][b]
    f = np.float32
    cw = inp["a_conv_w"][0]
    return {
        "xfull": np.ascontiguousarray(x),
        "xown": np.ascontiguousarray(x.reshape(16, 2, 128, D)[:, h].reshape(2048, D)),
        "mem": np.ascontiguousarray(inp["mem"][b]),
        "hsel": np.full((128, 1), float(h), f),
        "a_norm_g": np.ascontiguousarray(inp["a_norm_g"][0]),
        "a_w_in": np.ascontiguousarray(inp["a_w_in"][0]),
        "conv_w": np.ascontiguousarray(cw.reshape(4, 12, 128).transpose(2, 1, 0).reshape(128, 48)),
        "conv_b": np.ascontiguousarray(inp["a_conv_b"][0].reshape(12, 128).T),
        "gate_w": np.ascontiguousarray(inp["a_gate_w"][0].transpose(1, 0, 2).reshape(128, 12 * 256)),
        "gate_b": np.ascontiguousarray(inp["a_gate_b"][0].reshape(12, 2, 128).transpose(2, 0, 1).reshape(128, 24)),
        "lam": np.ascontiguousarray(inp["a_lambda"][0].reshape(12, 128).T),
        "a_w_out": np.ascontiguousarray(inp["a_w_out"][0]),
        "m_norm_g": np.ascontiguousarray(inp["m_norm_g"][0]),
        "m_w_kv": np.ascontiguousarray(inp["m_w_kv"][0]),
        "m_q_g": np.ascontiguousarray(inp["m_q_norm_g"][0].reshape(128, 1)),
        "m_k_g": np.ascontiguousarray(inp["m_k_norm_g"][0].reshape(128, 1)),
        "p_norm_g": np.ascontiguousarray(inp["p_norm_g"][0]),
        "p_w_q": np.ascontiguousarray(inp["p_w_q"][0]),
        "p_skT": np.ascontiguousarray(inp["p_subkeys"][0].transpose(3, 0, 1, 2).reshape(128, 2048)),
        "p_u": np.ascontiguousarray(inp["p_u"][0]),
        "p_v": np.ascontiguousarray(inp["p_v"][0]),
        "s_norm_g": np.ascontiguousarray(inp["s_norm_g"]),
        "s_w_kvf": np.ascontiguousarray(inp["s_w_kvf"]),
        "s_b_f": np.ascontiguousarray(inp["s_b_f"]),
        "s_k_g": np.ascontiguousarray(inp["s_k_norm_g"].reshape(128, 1)),
    }


def build_l2(debug=False):
    K = KB("l2")
    nc, P, A = K.nc, K.P, K.A
    xown = K.inp("xown", [2048, D])
    mem = K.inp("mem", [256, D])
    hsel_d = K.inp("hsel", [128, 1])
    kT_d = K.inp("kT", [12, 128, 4096], BF16)
    v_d = K.inp("v", [4096, 1536], BF16)
    lf_d = K.inp("lf", [4096, 12])
    gbn_d = K.inp("b_norm_g", [D])
    w_in = K.inp("b_w_in", [D, D])
    fqg_d = K.inp("b_q_g", [128, 1])
    w_out = K.inp("b_w_out", [D, D])
    gm_d = K.inp("m_norm_g", [D])
    wkv_d = K.inp("m_w_kv", [D, 1024])
    mqg_d = K.inp("m_q_g", [128, 1])
    mkg_d = K.inp("m_k_g", [128, 1])
    gp_d = K.inp("p_norm_g", [D])
    wq_d = K.inp("p_w_q", [D, D])
    skT_d = K.inp("p_skT", [128, 2048])
    u_d = K.inp("p_u", [16384, D])
    vv_d = K.inp("p_v", [16384, D])
    x3_d = K.out("x3", [2048, D])
    x4_d = K.out("x4", [2048, D])
    dbg = None
    if debug:
        dbg = {"idxT": K.out("d_idxT", [128, 2048], I32), "gT": K.out("d_gT", [128, 2048]),
               "main": K.out("d_main", [128, 16 * 2048], BF16)}

    K.consts()
    K.small()
    hsel = A.alloc([1], F32)
    mqg = A.alloc([1], F32)
    fqg = A.alloc([1], F32)
    K.dma(hsel, hsel_d, (), ["hsel"], key="small")
    K.dma(mqg, mqg_d, (), ["mqg"], key="small")
    K.dma(fqg, fqg_d, (), ["fqg0"], key="small")
    K.ts(mqg, mqg, 128.0 ** -0.5, ALU.mult, ["mqg"], ["qg"])
    K.ts(fqg, fqg, 128.0 ** -0.5, ALU.mult, ["fqg0"], ["fqg"])
    maskA = A.alloc([128], BF16)
    maskB = A.alloc([128], BF16)
    K.ts(maskA, K.iof, 0.0, ALU.is_ge, ["iof", "hsel"], ["maskA"], s2=hsel, op1=ALU.max)
    K.ts(maskB, K.iof, 0.0, ALU.is_ge, ["iof", "hsel"], ["maskB"], s2=hsel, op1=ALU.mult)
    c_sb = A.alloc([32, 12], F32)
    cref = A.alloc([16, 12], F32)
    K.mem_kv(mem, gm_d, wkv_d, mkg_d)
    mark = A.off
    lfS = A.alloc([32, 12], F32)
    tri = A.alloc([128], F32)
    cl = A.alloc([16, 2, 12], F32)
    cd = A.alloc([16, 12], F32)
    K.dma(lfS, lf_d.rearrange("(kb p) h -> p kb h", p=128), (), ["lfS"], key="small")
    K.ts(tri, K.iof, 0.0, ALU.is_ge, ["iof"], ["tri"])
    pc = K.bank(0)[:, 0:384]
    for kb in range(32):
        o = pc[:, kb * 12:(kb + 1) * 12]
        for k2 in range(kb):
            K.mm(o, K.onesf, lfS[:, k2, :], k2 == 0, False, ["onesf", "lfS"], ["ps0"])
        K.mm(o, tri, lfS[:, kb, :], kb == 0, True, ["tri", "lfS"], ["ps0"])
    K.cp(c_sb.rearrange("p a b -> p (a b)"), pc, ["ps0"], ["c_sb"])
    pl = K.bank(1)[:, 0:384]
    K.mm(pl, K.identf[:, 127:128].to_broadcast([128, 128]), c_sb.rearrange("p a b -> p (a b)"), True, True,
         ["identf", "c_sb"], ["ps1"])
    K.cp(cl.rearrange("p a b c -> p (a b c)"), pl, ["ps1"], ["cl"])
    K.tt(cd, cl[:, :, 1, :], cl[:, :, 0, :], ALU.subtract, ["cl"], ["cd"])
    K.stt(cref, cd, hsel, cl[:, :, 0, :], ALU.mult, ALU.add, ["cd", "hsel", "cl"], ["cref"])
    P.barrier()
    A.off = mark
    ab_base = A.off
    AB = A.alloc([16, 2048], BF16)
    base = A.off

    W1 = A.alloc([16, 2048], BF16)
    gb = A.alloc([2048], F32)
    xt = A.alloc([2, 2048], F32)
    hnp = A.alloc([2, 2048], BF16)
    hnT = A.alloc([16, 512], BF16)
    wk = (A.alloc([512], F32), A.alloc([512], F32), A.alloc([512], F32), A.alloc([512], BF16),
          A.alloc([2, 512], BF16), A.alloc([512], F32))
    K.dma(gb, gbn_d.partition_broadcast(128), (), ["gb"], key="gb")
    K.load_w(W1, w_in, 2048, 0, "W1")
    for q in range(4):
        for k in range(4):
            i = 4 * q + k
            s = i % 2
            K.dma(xt[:, s, :], xown[i * 128:(i + 1) * 128, :], (), ["x%d" % s], key="x%d" % s)
            K.norm_T(xt[:, s, :], "x%d" % s, gb, hnp[:, s, :], "hnp%d" % s,
                     lambda half, k=k: hnT[:, 8 * half:8 * half + 8, k * 128:(k + 1) * 128], "hnT", 0)
        for n in range(16):
            bk = n % 4
            ps = K.bank(bk)
            for c in range(16):
                K.mm(ps, W1[:, c, n * 128:(n + 1) * 128], hnT[:, c, :], c == 0, c == 15, ["W1", "hnT"], ["ps%d" % bk])
            dst = AB[:, n, q * 512:(q + 1) * 512]
            if n < 12:
                K.qnorm("ps%d" % bk, ps, fqg, dst, "AB", wk)
            else:
                K.mem_attn(n - 12, "ps%d" % bk, ps, mqg, dst, "AB", wk)
    P.barrier()
    A.off = base

    KT = A.alloc([2, 4096], BF16)
    VH = A.alloc([2, 32, 128], BF16)
    bias = A.alloc([2, 32], F32)
    pT = A.alloc([4, 128], BF16)
    rz = A.alloc([2, 128], F32)
    vview = v_d.rearrange("(kb p) c -> p kb c", p=128)
    for hh in range(12):
        hs = hh % 2
        K.dma(KT[:, hs, :], kT_d[hh], (), ["KT%d" % hs], key="KT%d" % hs)
        K.dma(VH[:, hs], vview[:, :, hh * 128:(hh + 1) * 128], (), ["VH%d" % hs], key="VH%d" % hs)
        for i in range(NOWN):
            nkb = 2 * i + 2
            bs_ = i % 2
            abk = "AB.%d.%d" % (hh, i)
            K.ts(bias[:, bs_, 0:nkb], c_sb[:, 0:nkb, hh], -1.0, ALU.mult, ["c_sb", "cref"], ["bias%d" % bs_],
                 s2=cref[:, i, hh:hh + 1], op1=ALU.add)
            K.ts(bias[:, bs_, 0:nkb], bias[:, bs_, 0:nkb], 0.0, ALU.min, ["bias%d" % bs_], ["bias%d" % bs_])
            qn = AB[:, hh, i * 128:(i + 1) * 128]
            po = K.bank(4 + 2 * bs_)[:, 0:128]
            pz = K.bank(5 + 2 * bs_)[:, 0:128]
            pok, pzk = "ps%d" % (4 + 2 * bs_), "ps%d" % (5 + 2 * bs_)
            for kb in range(nkb):
                sb = kb % 4
                ps = K.bank(sb)[:, 0:128]
                K.mm(ps, KT[:, hs, kb * 128:(kb + 1) * 128], qn, True, True, ["KT%d" % hs, abk], ["ps%d" % sb])
                K.act(pT[:, sb, :], ps, AF.Exp, ["ps%d" % sb, "bias%d" % bs_], ["pT%d" % sb],
                      bias=bias[:, bs_, kb:kb + 1], scale=1.0)
                if kb >= 2 * i:
                    mk = maskA if kb == 2 * i else maskB
                    K.tt(pT[:, sb, :], pT[:, sb, :], mk, ALU.mult, ["pT%d" % sb, "maskA", "maskB"], ["pT%d" % sb])
                K.mm(po, VH[:, hs, kb, :], pT[:, sb, :], kb == 0, kb == nkb - 1, ["VH%d" % hs, "pT%d" % sb], [pok])
                K.mm(pz, K.onesb, pT[:, sb, :], kb == 0, kb == nkb - 1, ["onesb", "pT%d" % sb], [pzk])
            K.recip(rz[:, bs_, :], pz, [pzk], ["rz%d" % bs_])
            K.tt(qn, po, rz[:, bs_, :], ALU.mult, [pok, "rz%d" % bs_], [abk])
    if debug:
        K.dma(dbg["main"], AB.rearrange("p a b -> p (a b)"), ["AB"] + ["AB.%d.%d" % (a, b) for a in range(12) for b in range(16)],
              [], key="dbg")
    P.barrier()
    A.off = base
    K.out_proj(AB, w_out, xown, x3_d, "none")
    K.peer(ab_base, x3_d, "x1d", x4_d, gp_d, wq_d, skT_d, u_d, vv_d, dbg)
    return K.finish([])


def l2_inputs(inp, c, x2, kT, v, lf):
    b, h = c // 2, c % 2
    f = np.float32
    return {
        "xown": x2,
        "mem": np.ascontiguousarray(inp["mem"][b]),
        "hsel": np.full((128, 1), float(h), f),
        "kT": kT, "v": v, "lf": lf,
        "b_norm_g": np.ascontiguousarray(inp["b_norm_g"][0]),
        "b_w_in": np.ascontiguousarray(inp["b_w_in"][0]),
        "b_q_g": np.ascontiguousarray(inp["b_q_norm_g"][0].reshape(128, 1)),
        "b_w_out": np.ascontiguousarray(inp["b_w_out"][0]),
        "m_norm_g": np.ascontiguousarray(inp["m_norm_g"][1]),
        "m_w_kv": np.ascontiguousarray(inp["m_w_kv"][1]),
        "m_q_g": np.ascontiguousarray(inp["m_q_norm_g"][1].reshape(128, 1)),
        "m_k_g": np.ascontiguousarray(inp["m_k_norm_g"][1].reshape(128, 1)),
        "p_norm_g": np.ascontiguousarray(inp["p_norm_g"][1]),
        "p_w_q": np.ascontiguousarray(inp["p_w_q"][1]),
        "p_skT": np.ascontiguousarray(inp["p_subkeys"][1].transpose(3, 0, 1, 2).reshape(128, 2048)),
        "p_u": np.ascontiguousarray(inp["p_u"][1]),
        "p_v": np.ascontiguousarray(inp["p_v"][1]),
    }


def interleave(a0, a1, axis):
    a0 = np.moveaxis(np.asarray(a0), axis, 0)
    a1 = np.moveaxis(np.asarray(a1), axis, 0)
    sh = a0.shape[1:]
    full = np.stack([a0.reshape(16, 128, *sh), a1.reshape(16, 128, *sh)], axis=1).reshape(4096, *sh)
    return np.ascontiguousarray(np.moveaxis(full, 0, axis))


def kernel(**inputs):
    inp = {k: np.asarray(v) for k, v in inputs.items()}
    n = 8
    nc1 = build_l1()
    maps1 = [l1_inputs(inp, c) for c in range(n)]
    r1 = run_bass_kernel_spmd(nc1, maps1, core_ids=list(range(n))).results
    del maps1
    maps2 = []
    for b in range(4):
        kT = interleave(r1[2 * b]["kT"], r1[2 * b + 1]["kT"], 2)
        v = interleave(r1[2 * b]["v"], r1[2 * b + 1]["v"], 0)
        lf = interleave(r1[2 * b]["lf"], r1[2 * b + 1]["lf"], 0)
        for h in range(2):
            maps2.append(l2_inputs(inp, 2 * b + h, np.ascontiguousarray(r1[2 * b + h]["x2"]), kT, v, lf))
    nc2 = build_l2()
    r2 = run_bass_kernel_spmd(nc2, maps2, core_ids=list(range(n))).results
    out = np.empty((4, 4096, D), np.float32)
    for c in range(n):
        b, h = c // 2, c % 2
        out[b].reshape(16, 2, 128, D)[:, h] = np.asarray(r2[c]["x4"]).reshape(16, 128, D)
    return out
```
